# Optimizing a Trainium2 kernel written in Bass

```python
import jax, jax.numpy as jnp
from jax import lax
import numpy as np

D_MODEL = 1024
BATCH = 8
SEQ = 4096
DEPTH = 1

PLE_DIM = 256
D_FF = 2816
D_MIX = D_MODEL
CHUNK = 128
GM_HEADS = 4
GM_HEAD_DIM = 128
GM_WIDTH = GM_HEADS * GM_HEAD_DIM
SB_HEADS = 8
SB_HEAD_DIM = 64
SB_WIDTH = SB_HEADS * SB_HEAD_DIM
SB_BLOCK = 128
MIX_IN_WIDTH = 2 * GM_WIDTH + 3 * SB_WIDTH
EPS = 1e-6

kernel_name = "hybrid_gmlp_stickbreaking_macaron_block"


def rms_norm(x, g):
    xf = x.astype(jnp.float32)
    y = xf * lax.rsqrt(jnp.mean(xf * xf, axis=-1, keepdims=True) + EPS)
    return (y * g.astype(jnp.float32)).astype(x.dtype)


def swiglu(x, w_in, w_out):
    gate, up = jnp.split(x @ w_in, 2, axis=-1)
    return (jax.nn.silu(gate) * up) @ w_out


def chunked_gmlp(u, v, v_gain, w_s, b_s):
    B, S, _ = u.shape
    nc = S // CHUNK
    vn = rms_norm(v, v_gain).reshape(B, nc, CHUNK, GM_HEADS, GM_HEAD_DIM)
    causal = jnp.tril(jnp.ones((CHUNK, CHUNK), dtype=bool))
    w = jnp.where(causal[None], w_s, jnp.zeros_like(w_s)).astype(vn.dtype)
    sv = jnp.einsum('hts,bcshd->bcthd', w, vn) + b_s.T.astype(vn.dtype)[None, None, :, :, None]
    return u * sv.reshape(B, S, GM_WIDTH)


def stick_breaking_attention(q, k, v):
    B, S, H, D = q.shape
    scale = D ** -0.5
    outs = []
    for i in range(S // SB_BLOCK):
        q0 = i * SB_BLOCK
        L = q0 + SB_BLOCK
        qb = q[:, q0:L]
        kp = k[:, :L]
        vp = v[:, :L]
        z = jnp.einsum('bthd,bshd->bhts', qb, kp).astype(jnp.float32) * scale
        t_idx = q0 + jnp.arange(SB_BLOCK)[:, None]
        s_idx = jnp.arange(L)[None, :]
        causal = s_idx < t_idx
        log_1m = jnp.where(causal, -jax.nn.softplus(z), 0.0)
        after = lax.cumsum(log_1m, axis=3, reverse=True) - log_1m
        a = jnp.where(causal, jnp.exp(jax.nn.log_sigmoid(z) + after), 0.0)
        outs.append(jnp.einsum('bhts,bshd->bthd', a.astype(vp.dtype), vp))
    return jnp.concatenate(outs, axis=1)


def setup_inputs(seed: int = 0) -> dict:
    key = jax.random.key(seed)
    ks = jax.random.split(key, 18)
    f32 = jnp.float32

    def nrm(k, shape, fan_in):
        return jax.random.normal(k, shape, f32) * (fan_in ** -0.5)

    def gain(k, shape):
        return jnp.ones(shape, f32) + 0.05 * jax.random.normal(k, shape, f32)

    return {
        "x": jax.random.normal(ks[0], (BATCH, SEQ, D_MODEL), f32),
        "p": jax.random.normal(ks[1], (DEPTH, BATCH, SEQ, PLE_DIM), f32),
        "ffn1_norm": gain(ks[2], (DEPTH, D_MODEL)),
        "ffn1_w_in": nrm(ks[3], (DEPTH, D_MODEL, 2 * D_FF), D_MODEL),
        "ffn1_w_out": nrm(ks[4], (DEPTH, D_FF, D_MODEL), D_FF),
        "mix_norm": gain(ks[5], (DEPTH, D_MODEL)),
        "w_mix_in": nrm(ks[6], (DEPTH, D_MODEL, MIX_IN_WIDTH), D_MODEL),
        "gmlp_v_norm": gain(ks[7], (DEPTH, GM_WIDTH)),
        "gmlp_w_s": nrm(ks[8], (DEPTH, GM_HEADS, CHUNK, CHUNK), CHUNK),
        "gmlp_b": jnp.ones((DEPTH, GM_HEADS, CHUNK), f32) + 0.1 * jax.random.normal(ks[9], (DEPTH, GM_HEADS, CHUNK), f32),
        "w_mix_out": nrm(ks[10], (DEPTH, D_MIX, D_MODEL), D_MIX),
        "ffn2_norm": gain(ks[11], (DEPTH, D_MODEL)),
        "ffn2_w_in": nrm(ks[12], (DEPTH, D_MODEL, 2 * D_FF), D_MODEL),
        "ffn2_w_out": nrm(ks[13], (DEPTH, D_FF, D_MODEL), D_FF),
        "ple_norm": gain(ks[14], (DEPTH, D_MODEL)),
        "ple_w_gate": nrm(ks[15], (DEPTH, D_MODEL, D_MODEL), D_MODEL),
        "ple_w_proj": nrm(ks[16], (DEPTH, PLE_DIM, D_MODEL), PLE_DIM),
        "final_norm": gain(ks[17], (D_MODEL,)),
    }


def reference(x, p, ffn1_norm, ffn1_w_in, ffn1_w_out, mix_norm, w_mix_in, gmlp_v_norm,
              gmlp_w_s, gmlp_b, w_mix_out, ffn2_norm, ffn2_w_in, ffn2_w_out,
              ple_norm, ple_w_gate, ple_w_proj, final_norm):
    B, S, _ = x.shape
    splits = [GM_WIDTH, 2 * GM_WIDTH, 2 * GM_WIDTH + SB_WIDTH, 2 * GM_WIDTH + 2 * SB_WIDTH]
    h = x
    for i in range(DEPTH):
        h = h + 0.5 * swiglu(rms_norm(h, ffn1_norm[i]), ffn1_w_in[i], ffn1_w_out[i])

        n = rms_norm(h, mix_norm[i])
        zmix = n @ w_mix_in[i]
        gm_u, gm_v, sb_q, sb_k, sb_v = jnp.split(zmix, splits, axis=-1)
        gm_out = chunked_gmlp(jax.nn.gelu(gm_u, approximate=False),
                              jax.nn.gelu(gm_v, approximate=False),
                              gmlp_v_norm[i], gmlp_w_s[i], gmlp_b[i])
        sb_out = stick_breaking_attention(sb_q.reshape(B, S, SB_HEADS, SB_HEAD_DIM),
                                          sb_k.reshape(B, S, SB_HEADS, SB_HEAD_DIM),
                                          sb_v.reshape(B, S, SB_HEADS, SB_HEAD_DIM))
        mixed = jnp.concatenate([gm_out, sb_out.reshape(B, S, SB_WIDTH)], axis=-1)
        h = h + mixed @ w_mix_out[i]

        h = h + 0.5 * swiglu(rms_norm(h, ffn2_norm[i]), ffn2_w_in[i], ffn2_w_out[i])

        gate = jax.nn.sigmoid(rms_norm(h, ple_norm[i]) @ ple_w_gate[i])
        h = h + gate * (p[i] @ ple_w_proj[i])
    return rms_norm(h, final_norm)
```

```python
import numpy as np
import concourse.bass as bass
import concourse.mybir as mybir
from concourse.bass_utils import run_bass_kernel_spmd

F32 = mybir.dt.float32
BF16 = mybir.dt.bfloat16
AF = mybir.ActivationFunctionType
ALU = mybir.AluOpType

D = 1024
DFF = 2816
NFF = DFF // 128
T = 512
EPS = 1e-6
RING = 12
NSTAGE = 4
ENGS = ("pe", "act", "dve", "pool", "sp")


class Res:
    __slots__ = ("name", "w", "r")

    def __init__(self, name):
        self.name = name
        self.w = None
        self.r = {}


class Op:
    __slots__ = ("eng", "seq", "fn", "waits", "needed", "dma", "val")


class Sched:
    def __init__(self):
        self.q = {e: [] for e in ENGS}
        self.ncomp = {e: 0 for e in ENGS}
        self.obs = {e: {} for e in ENGS}
        self.dma_cnt = {}

    def op(self, eng, fn, reads=(), writes=(), dma=None, extra=()):
        o = Op()
        o.eng = eng
        o.fn = fn
        o.dma = dma
        o.needed = False
        o.val = 0
        waits = {}
        obs = self.obs[eng]

        def need(p):
            if p.dma is None:
                key = p.eng
                if p.eng == eng and eng == "pe" and dma is None:
                    return
            else:
                key = "dma:" + p.dma
            if obs.get(key, 0) >= p.seq:
                return
            cur = waits.get(key)
            if cur is None or cur.seq < p.seq:
                waits[key] = p

        for r in reads:
            if r.w is not None:
                need(r.w)
        for w in writes:
            if w.w is not None:
                need(w.w)
            for p in w.r.values():
                need(p)
        for p in extra:
            need(p)
        for key, p in waits.items():
            obs[key] = p.seq
            p.needed = True
        o.waits = list(waits.values())
        if dma is None:
            self.ncomp[eng] += 1
            o.seq = self.ncomp[eng]
            key = eng
        else:
            self.dma_cnt[dma] = self.dma_cnt.get(dma, 0) + 1
            o.seq = self.dma_cnt[dma]
            key = "dma:" + dma
        for r in reads:
            cur = r.r.get(key)
            if cur is None or cur.seq < o.seq:
                r.r[key] = o
        for w in writes:
            w.w = o
            w.r = {}
        self.q[eng].append(o)
        return o

    def emit(self, nc, block):
        sems = {e: nc.alloc_semaphore("s_" + e) for e in ENGS}
        dsems = {c: nc.alloc_semaphore("d_" + c) for c in self.dma_cnt}
        for e in ENGS:
            v = 0
            for o in self.q[e]:
                if o.dma is None and o.needed:
                    v += 1
                    o.val = v

        def runner(ename):
            def f(eng):
                for o in self.q[ename]:
                    for p in o.waits:
                        if p.dma is None:
                            eng.wait_ge(sems[p.eng], p.val)
                        else:
                            eng.wait_ge(dsems[p.dma], 16 * p.seq)
                    ins = o.fn(eng)
                    if o.dma is not None:
                        ins.then_inc(dsems[o.dma], 16)
                    elif o.needed:
                        ins.then_inc(sems[ename], 1)
            return f

        block.tensor(runner("pe"))
        block.scalar(runner("act"))
        block.vector(runner("dve"))
        block.gpsimd(runner("pool"))
        block.sync(runner("sp"))


class Buf:
    __slots__ = ("ap", "res", "chan")

    def __init__(self, ap, res, chan=None):
        self.ap = ap
        self.res = res if isinstance(res, list) else [res]
        self.chan = chan


def f_mm(out, lhsT, rhs, start, stop, skip=False):
    if skip:
        return lambda e: e.matmul(out, lhsT, rhs, start=start, stop=stop, skip_group_check=True)
    return lambda e: e.matmul(out, lhsT, rhs, start=start, stop=stop)


def f_tr(out, in_, ident):
    return lambda e: e.transpose(out, in_, ident)


def f_act(out, in_, func, bias=None, scale=None, accum_out=None):
    kw = {}
    if bias is not None:
        kw["bias"] = bias
    if scale is not None:
        kw["scale"] = scale
    if accum_out is not None:
        kw["accum_out"] = accum_out
    return lambda e: e.activation(out, in_, func, **kw)


def f_copy(out, in_):
    return lambda e: e.tensor_copy(out, in_)


def f_acopy(out, in_):
    return lambda e: e.activation(out, in_, AF.Copy)


def f_tt(out, in0, in1, op):
    return lambda e: e.tensor_tensor(out, in0, in1, op)


def f_ts(out, in0, s1, s2, op0, op1=None):
    if op1 is None:
        return lambda e: e.tensor_scalar(out, in0, s1, None, op0)
    return lambda e: e.tensor_scalar(out, in0, s1, s2, op0, op1)


def f_stt(out, in0, scalar, in1, op0, op1):
    return lambda e: e.scalar_tensor_tensor(out, in0, scalar, in1, op0, op1)


def f_memset(ap, v):
    return lambda e: e.memset(ap, v)


def f_dma(out, in_):
    return lambda e: e.dma_start(out=out, in_=in_)


def tile_unit_order():
    order = []
    for fi in (0, 1):
        pass
    def ffn_units(fi):
        u = []
        for c in range(NFF):
            u.append(("in", fi, c, 0))
            u.append(("in", fi, c, 1))
        for j in range(8):
            for piece in range(3):
                u.append(("out", fi, j, piece))
        return u
    order += ffn_units(0)
    for oc in range(4):
        order.append(("mi_u", oc))
    for kp in range(4):
        order.append(("mi_gv", kp))
    for qc in range(4):
        order.append(("mi_q", qc))
    for kc in range(4):
        order.append(("mi_k", kc))
    for kp in range(4):
        order.append(("mi_v", kp))
    for j in range(8):
        order.append(("mo", j))
    order += ffn_units(1)
    for j in range(8):
        if j % 4 == 0:
            order.append(("pp", j // 4))
        order.append(("pg", j))
    return order


def build(NT, interleave_on=True):
    nc = bass.Bass("TRN2", target_bir_lowering=False)
    S_TOK = NT * T

    def din(name, shape):
        return nc.dram_tensor(name, shape, F32, kind="ExternalInput").ap()

    x_d = din("x", [S_TOK, D])
    p_d = din("p", [S_TOK, 256])
    norms_d = din("norms", [40, 128])
    w_in_d = [din("f1_w_in", [D, 2 * DFF]), din("f2_w_in", [D, 2 * DFF])]
    w_out_d = [din("f1_w_out", [DFF, D]), din("f2_w_out", [DFF, D])]
    w_mi_d = din("w_mix_in", [D, 2560])
    w_mo_d = din("w_mix_out", [D, D])
    w_pg_d = din("ple_w_gate", [D, D])
    w_pp_d = din("ple_w_proj", [256, D])
    gvn_d = din("gmlp_v_norm", [512])
    gws_d = din("gmlp_w_s", [4, 128, 128])
    gb_d = din("gmlp_b", [512])
    cst_d = din("consts", [128, 640])
    out_d = nc.dram_tensor("out", [S_TOK, D], F32, kind="ExternalOutput").ap()

    order = tile_unit_order()
    NU = len(order)
    uidx = {u: i for i, u in enumerate(order)}
    wscr = nc.dram_tensor("wscr", [NU, 128, 1024], BF16).ap()

    def unit_src(u):
        kind = u[0]
        if kind == "in":
            _, fi, c, g = u
            v = w_in_d[fi].rearrange("(k p) n -> p k n", p=128)
            c0 = g * DFF + c * 128
            return v[:, :, c0:c0 + 128], 8, 128
        if kind == "out":
            _, fi, j, piece = u
            v = w_out_d[fi].rearrange("(c p) n -> p c n", p=128)
            c0 = piece * 8
            c1 = min(c0 + 8, NFF)
            return v[:, c0:c1, j * 128:(j + 1) * 128], c1 - c0, 128
        if kind in ("mi_u", "mi_q", "mi_k"):
            base = {"mi_u": 0, "mi_q": 1024, "mi_k": 1536}[kind]
            v = w_mi_d.rearrange("(k p) n -> p k n", p=128)
            c0 = base + u[1] * 128
            return v[:, :, c0:c0 + 128], 8, 128
        if kind in ("mi_gv", "mi_v"):
            base = {"mi_gv": 512, "mi_v": 2048}[kind]
            v = w_mi_d.rearrange("(k p) n -> p k n", p=128)
            kp = u[1]
            return v[:, 2 * kp:2 * kp + 2, base:base + 512], 2, 512
        if kind == "mo":
            v = w_mo_d.rearrange("(k p) n -> p k n", p=128)
            return v[:, :, u[1] * 128:(u[1] + 1) * 128], 8, 128
        if kind == "pg":
            v = w_pg_d.rearrange("(k p) n -> p k n", p=128)
            return v[:, :, u[1] * 128:(u[1] + 1) * 128], 8, 128
        if kind == "pp":
            v = w_pp_d.rearrange("(k p) n -> p k n", p=128)
            return v[:, :, u[1] * 512:(u[1] + 1) * 512], 2, 512
        raise ValueError(u)

    off = [(nc.sbuf_base + 63) // 64 * 64]
    top = nc.sbuf_top

    def salloc(name, shape, dtype, at=None):
        esz = 4 if dtype == F32 else 2
        n = 1
        for s in shape[1:]:
            n *= s
        nbytes = n * esz
        if at is None:
            o = off[0]
            off[0] += (nbytes + 63) // 64 * 64
            assert off[0] <= top, "SBUF overflow %d > %d" % (off[0], top)
        else:
            o = at
        return nc.alloc_sbuf_tensor_at(name, shape, dtype, offset=o), o

    cst, _ = salloc("cst", [128, 640], F32)
    cb, _ = salloc("cb", [128, 640], BF16)
    negones, _ = salloc("negones", [128, 128], BF16)
    graw, _ = salloc("graw", [40, 128], F32)
    g32, _ = salloc("g32", [128, 40], F32)
    gvg, _ = salloc("gvg", [128, 512], F32)
    bbc, _ = salloc("bbc", [128, 4, 128], F32)
    wsT, _ = salloc("wsT", [128, 4, 128], BF16)
    ssq, _ = salloc("ssq", [128, 4], F32)
    rs1, _ = salloc("rs1", [128, 4], F32)
    biasc, _ = salloc("biasc", [128, 4], F32)
    kT, kT_off = salloc("kT", [128, 4, 8 * T], BF16)
    vc, vc_off = salloc("vc", [128, 32, 512], BF16)
    qpad, _ = salloc("qpad", [128, 8, T], BF16)
    hTb = [salloc("hT%d" % i, [128, 8, T], F32)[0] for i in range(2)]
    xnT, _ = salloc("xnT", [128, 8, T], BF16)
    mixedT, _ = salloc("mixedT", [128, 8, T], BF16)
    pT, _ = salloc("pT", [128, 2, T], BF16)
    hid, scr_off = salloc("hid", [128, NFF, T], BF16)
    efb = [salloc("ef%d" % i, [128, T], F32)[0] for i in range(2)]
    sqb = [salloc("sq%d" % i, [128, T], BF16)[0] for i in range(2)]
    rr, _ = salloc("rr", [128, T], F32)
    stg0_off = off[0]
    stg = [salloc("stg%d" % i, [128, T], F32)[0] for i in range(NSTAGE)]
    wsm = salloc("wsm", [128, 4, 128], F32, at=stg0_off)[0]
    ring = [salloc("ring%d" % i, [128, 1024], BF16)[0] for i in range(RING)]
    Eb_t = [salloc("E%d" % i, [128, T], BF16)[0] for i in range(2)]
    SPb_t = [salloc("SP%d" % i, [128, T], BF16)[0] for i in range(4)]
    RSb_t = [salloc("RS%d" % i, [128, T], BF16)[0] for i in range(2)]
    Ab_t = [salloc("A%d" % i, [128, T], BF16)[0] for i in range(4)]
    Eb = [Buf(t[:, :], Res("E%d" % i)) for i, t in enumerate(Eb_t)]
    SPb = [Buf(t[:, :], Res("SP%d" % i)) for i, t in enumerate(SPb_t)]
    RSb = [Buf(t[:, :], Res("RS%d" % i)) for i, t in enumerate(RSb_t)]
    Ab = [Buf(t[:, :], Res("A%d" % i)) for i, t in enumerate(Ab_t)]
    scr_res = [Res("scr%d" % i) for i in range(NFF)]
    guT = salloc("guT", [128, 4, T], BF16, at=scr_off)[0]
    guT_res = [scr_res[i] for i in range(4)]
    gvn = salloc("gvn", [128, 4, T], BF16, at=scr_off + 4096)[0]
    gvn_res = [scr_res[4 + i] for i in range(4)]
    print("SBUF used %d of %d" % (off[0], top))

    banks = [nc.alloc_psum_tensor("bank%d" % i, [128, 512], F32) for i in range(8)]
    bank_res = [Res("bank%d" % i) for i in range(8)]

    R_const = Res("const")
    R_g = Res("g32")
    hT_resb = [[Res("hT%d_%d" % (i, k)) for k in range(8)] for i in range(2)]
    xnT_res = [Res("xnT%d" % k) for k in range(8)]
    mixedT_res = [Res("mixedT%d" % k) for k in range(8)]
    qpad_res = [Res("qpad%d" % k) for k in range(8)]
    kT_res = [[Res("kT%d_%d" % (k, t)) for t in range(8)] for k in range(4)]
    v_res = [Res("v%d" % b) for b in range(32)]
    pT_res = [Res("pT%d" % k) for k in range(2)]
    ef_res = [Res("ef%d" % i) for i in range(2)]
    gvt = efb[0]
    gvt_res = ef_res[0]
    sq_res = [Res("sq%d" % i) for i in range(2)]
    rr_res = Res("rr")
    stg_res = [Res("stg%d" % i) for i in range(NSTAGE)]
    ring_res = [Res("ring%d" % i) for i in range(RING)]
    ssq_res = Res("ssq")
    rs1_res = Res("rs1")
    misc_res = Res("misc")
    wscr_res = [Res("wscr%d" % i) for i in range(NU)]

    identf = cst[:, 0:128]
    wsmask = cst[:, 384:512]
    identb = cb[:, 0:128]
    mneg = cb[:, 128:256]
    negmask = cb[:, 256:384]
    ones_bf = cb[:, 512:640]

    NF = 8
    fst = []
    for i in range(4):
        fst.append((salloc("fstk%d" % i, [128, 1024], F32, at=kT_off + i * 8192 + 4096)[0], [kT_res[i][t] for t in range(4, 8)]))
    for i in range(4):
        fst.append((salloc("fstv%d" % i, [128, 1024], F32, at=vc_off + (16 + 4 * i) * 1024)[0], [v_res[16 + 4 * i + t] for t in range(4)]))
    cast_eng = ("dve", "pool", "act")

    class DrySched:
        def op(self, *a, **k):
            return None

    class DryRing:
        def __init__(self):
            self.stream = []

        def get(self, u):
            self.stream.append(u)
            return Buf(ring[0], ring_res[0])

        def prime(self):
            pass

    counts = {}

    def program(S, rg, dry):
        class BankPool:
            def __init__(self, ids):
                self.ids = ids
                self.i = 0

            def next(self):
                b = self.ids[self.i % len(self.ids)]
                self.i += 1
                return Buf(banks[b][:, :], bank_res[b])

        allbanks = BankPool(list(range(8)))
        sidebanks = BankPool([5, 6, 7])
        zbanks = BankPool([0, 1, 2, 3])
        avbanks = [Buf(banks[4][:, :], bank_res[4]), Buf(banks[4][:, :], bank_res[4])]
        stage_i = [0]
        evac_i = [0]

        def stage_next():
            i = stage_i[0] % NSTAGE
            stage_i[0] += 1
            return Buf(stg[i], stg_res[i], "s%d" % i)

        mode = {"inter": False}

        def evac_eng():
            evac_i[0] += 1
            if mode["inter"]:
                return "dve"
            return "act" if evac_i[0] % 2 else "dve"

        def emit_copy(eng, out, in_, reads, writes):
            if eng == "act":
                S.op("act", f_acopy(out, in_), reads=reads, writes=writes)
            else:
                S.op(eng, f_copy(out, in_), reads=reads, writes=writes)

        S.op("sp", f_dma(cst[:, :], cst_d[:, :]), writes=[R_const], dma="c0")
        S.op("sp", f_dma(graw[:, :], norms_d[:, :]), writes=[R_g], dma="c1")
        S.op("sp", f_dma(gvg[:, :], gvn_d.partition_broadcast(128)), writes=[misc_res], dma="c2")
        S.op("sp", f_dma(bbc[:, :, :].rearrange("p h t -> p (h t)"), gb_d.partition_broadcast(128)), writes=[misc_res], dma="c3")
        S.op("sp", f_dma(wsm[:, :, :], gws_d.rearrange("h t s -> t h s")), writes=[stg_res[0]], dma="s0")
        rg.prime()
        S.op("dve", f_copy(cb[:, :], cst[:, :]), reads=[R_const], writes=[R_const])
        S.op("dve", f_ts(negones[:, :], cst[:, 512:640], -1.0, None, ALU.mult), reads=[R_const], writes=[R_const])
        S.op("dve", f_memset(biasc[:, 0:1], 1024.0 * EPS), writes=[R_const])
        S.op("dve", f_memset(biasc[:, 1:2], 512.0 * EPS), writes=[R_const])
        S.op("dve", f_memset(biasc[:, 2:3], 1.0), writes=[R_const])
        S.op("pool", f_memset(qpad[:, :, :].rearrange("p h t -> p (h t)"), 0.0), writes=qpad_res)
        b = allbanks.next()
        S.op("pe", f_tr(b.ap[:, 0:40], graw[:, :], cst[0:40, 0:40]), reads=[R_g, R_const], writes=b.res)
        S.op("dve", f_ts(g32[:, :], b.ap[:, 0:40], 32.0, None, ALU.mult), reads=b.res, writes=[R_g])
        S.op("dve", f_ts(gvg[:, :], gvg[:, :], float(np.sqrt(512.0)), None, ALU.mult), reads=[misc_res], writes=[misc_res])
        for h in range(4):
            S.op("dve", f_tt(wsm[:, h, :], wsm[:, h, :], wsmask, ALU.mult), reads=[stg_res[0], R_const], writes=[stg_res[0]])
        b = allbanks.next()
        for h in range(4):
            S.op("pe", f_tr(b.ap[:, h * 128:(h + 1) * 128], wsm[:, h, :], identf), reads=[stg_res[0], R_const], writes=b.res)
        S.op("dve", f_copy(wsT[:, :, :].rearrange("p h t -> p (h t)"), b.ap), reads=b.res, writes=[misc_res])
        stage_i[0] = 1

        def norm(ni, hb, bp, final=False):
            hT = hTb[hb]
            hT_res = hT_resb[hb]
            ssb = bp.next()
            for k in range(8):
                S.op("act", f_act(sqb[k % 2][:, :], hT[:, k, :], AF.Square), reads=[hT_res[k]], writes=[sq_res[k % 2]])
                S.op("pe", f_mm(ssb.ap, ones_bf, sqb[k % 2][:, :], k == 0, k == 7), reads=[sq_res[k % 2], R_const], writes=ssb.res)
            S.op("act", f_act(rr[:, :], ssb.ap, AF.Ln, bias=biasc[:, 0:1]), reads=ssb.res + [R_const], writes=[rr_res])
            S.op("act", f_act(rr[:, :], rr[:, :], AF.Exp, scale=-0.5), reads=[rr_res], writes=[rr_res])
            for k in range(8):
                gcol = g32[:, ni * 8 + k:ni * 8 + k + 1]
                if final:
                    S.op("dve", f_stt(hT[:, k, :], hT[:, k, :], gcol, rr[:, :], ALU.mult, ALU.mult),
                         reads=[hT_res[k], rr_res, R_g], writes=[hT_res[k]])
                else:
                    S.op("dve", f_stt(xnT[:, k, :], hT[:, k, :], gcol, rr[:, :], ALU.mult, ALU.mult),
                         reads=[hT_res[k], rr_res, R_g], writes=[xnT_res[k]])

        def proj8(w, src, src_res, bank):
            for k in range(8):
                S.op("pe", f_mm(bank.ap, w.ap[:, k * 128:(k + 1) * 128], src[:, k, :], k == 0, k == 7),
                     reads=w.res + [src_res[k]], writes=bank.res)

        def ffn_gen(fi, ni, hb, bp, exp_silu):
            hT = hTb[hb]
            hT_res = hT_resb[hb]
            norm(ni, hb, bp)
            yield
            for c in range(NFF):
                wg = rg.get(("in", fi, c, 0))
                wu = rg.get(("in", fi, c, 1))
                bg = bp.next()
                bu = bp.next()
                for wq, bq in ((wg, bg), (wu, bu)):
                    for k in range(8):
                        S.op("pe", f_mm(bq.ap, wq.ap[:, k * 128:(k + 1) * 128], xnT[:, k, :], k == 0, k == 7),
                             reads=wq.res + [xnT_res[k]], writes=bq.res)
                        if k % 4 == 3 and not (wq is wu and k == 7):
                            yield
                i = c % 2
                ef = efb[i][:, :]
                er = [ef_res[i]]
                if exp_silu:
                    S.op("act", f_act(ef, bg.ap, AF.Exp, scale=-1.0), reads=bg.res, writes=er)
                    S.op("act", f_act(ef, ef, AF.Ln, bias=biasc[:, 2:3]), reads=er + [R_const], writes=er)
                    S.op("act", f_act(ef, ef, AF.Exp, scale=-1.0), reads=er, writes=er)
                    S.op("dve", f_tt(ef, ef, bg.ap, ALU.mult), reads=er + bg.res, writes=er)
                else:
                    S.op("act", f_act(ef, bg.ap, AF.Silu), reads=bg.res, writes=er)
                S.op("dve", f_tt(hid[:, c, :], ef, bu.ap, ALU.mult), reads=er + bu.res, writes=[scr_res[c]])
                yield
            for j in range(8):
                by = bp.next()
                for piece in range(3):
                    w = rg.get(("out", fi, j, piece))
                    c0 = piece * 8
                    c1 = min(c0 + 8, NFF)
                    for c in range(c0, c1):
                        ci = c - c0
                        S.op("pe", f_mm(by.ap, w.ap[:, ci * 128:(ci + 1) * 128], hid[:, c, :], c == 0, c == NFF - 1),
                             reads=w.res + [scr_res[c]], writes=by.res)
                        if c % 4 == 3:
                            yield
                S.op("dve", f_stt(hT[:, j, :], by.ap, 0.5, hT[:, j, :], ALU.mult, ALU.add), reads=by.res + [hT_res[j]], writes=[hT_res[j]])
                yield

        def load_x_gen(ti, bp):
            hb = ti % 2
            hT = hTb[hb]
            hT_res = hT_resb[hb]
            for tb in range(4):
                r0 = ti * T + tb * 128
                for half in range(2):
                    st = stage_next()
                    S.op("sp", f_dma(st.ap[:, :], x_d[r0:r0 + 128, half * 512:(half + 1) * 512]), writes=st.res, dma=st.chan)
                    bk = bp.next()
                    for kk in range(4):
                        S.op("pe", f_tr(bk.ap[:, kk * 128:(kk + 1) * 128], st.ap[:, kk * 128:(kk + 1) * 128], identf),
                             reads=st.res + [R_const], writes=bk.res)
                    emit_copy(evac_eng(), hT[:, half * 4:(half + 1) * 4, tb * 128:(tb + 1) * 128],
                              bk.ap.rearrange("p (a b) -> p a b", a=4), bk.res, [hT_res[half * 4 + kk] for kk in range(4)])
                    yield

        def kv_gen(ti, bp):
            for kc in range(4):
                w = rg.get(("mi_k", kc))
                bk = bp.next()
                proj8(w, xnT, xnT_res, bk)
                emit_copy(evac_eng(), kT[:, kc, ti * T:(ti + 1) * T], bk.ap, bk.res, [kT_res[kc][ti]])
                yield
            wv = [rg.get(("mi_v", kp)) for kp in range(4)]
            for tb in range(4):
                bk = bp.next()
                for k in range(8):
                    w = wv[k // 2]
                    S.op("pe", f_mm(bk.ap, xnT[:, k, tb * 128:(tb + 1) * 128], w.ap[:, (k % 2) * 512:(k % 2 + 1) * 512], k == 0, k == 7),
                         reads=w.res + [xnT_res[k]], writes=bk.res)
                emit_copy(evac_eng(), vc[:, ti * 4 + tb, :], bk.ap, bk.res, [v_res[ti * 4 + tb]])
                yield

        def front_gen(ti, bp, exp_silu):
            for _ in load_x_gen(ti, bp):
                yield
            for _ in ffn_gen(0, 0, ti % 2, bp, exp_silu):
                yield
            norm(1, ti % 2, bp)
            yield
            for _ in kv_gen(ti, bp):
                yield

        def run(gen):
            n = 0
            for _ in gen:
                n += 1
            return n

        def mix_front(ti):
            hb = ti % 2
            bp = allbanks
            for oc in range(4):
                w = rg.get(("mi_u", oc))
                bk = bp.next()
                proj8(w, xnT, xnT_res, bk)
                S.op("act", f_act(guT[:, oc, :], bk.ap, AF.Gelu), reads=bk.res, writes=[guT_res[oc]])
            wv = [rg.get(("mi_gv", kp)) for kp in range(4)]
            for tb in range(4):
                bk = bp.next()
                for k in range(8):
                    w = wv[k // 2]
                    S.op("pe", f_mm(bk.ap, xnT[:, k, tb * 128:(tb + 1) * 128], w.ap[:, (k % 2) * 512:(k % 2 + 1) * 512], k == 0, k == 7),
                         reads=w.res + [xnT_res[k]], writes=bk.res)
                S.op("act", f_act(gvn[:, tb, :], bk.ap, AF.Gelu), reads=bk.res, writes=[gvn_res[tb]])
                S.op("act", f_act(sqb[tb % 2][:, :], gvn[:, tb, :], AF.Square, accum_out=ssq[:, tb:tb + 1]),
                     reads=[gvn_res[tb]], writes=[sq_res[tb % 2], ssq_res])
            S.op("act", f_act(rs1[:, 0:4], ssq[:, 0:4], AF.Ln, bias=biasc[:, 1:2]), reads=[ssq_res, R_const], writes=[rs1_res])
            S.op("act", f_act(rs1[:, 0:4], rs1[:, 0:4], AF.Exp, scale=-0.5), reads=[rs1_res], writes=[rs1_res])
            for tb in range(4):
                S.op("dve", f_stt(gvn[:, tb, :], gvn[:, tb, :], rs1[:, tb:tb + 1], gvg[:, :], ALU.mult, ALU.mult),
                     reads=[gvn_res[tb], rs1_res, misc_res], writes=[gvn_res[tb]])
            for qc in range(4):
                w = rg.get(("mi_q", qc))
                bk = bp.next()
                proj8(w, xnT, xnT_res, bk)
                S.op("act", f_act(qpad[0:64, 2 * qc, :], bk.ap[0:64, :], AF.Copy, scale=0.125), reads=bk.res, writes=[qpad_res[2 * qc]])
                S.op("dve", f_ts(qpad[64:128, 2 * qc + 1, :], bk.ap[64:128, :], 0.125, None, ALU.mult), reads=bk.res, writes=[qpad_res[2 * qc + 1]])
            for h in range(4):
                bk = bp.next()
                for tb in range(4):
                    S.op("pe", f_mm(bk.ap[:, tb * 128:(tb + 1) * 128], gvn[:, tb, h * 128:(h + 1) * 128], wsT[:, h, :], tb == 0, tb == 3),
                         reads=[gvn_res[tb], misc_res], writes=bk.res)
                for tb in range(4):
                    S.op("dve", f_tt(gvt[:, tb * 128:(tb + 1) * 128], bk.ap[:, tb * 128:(tb + 1) * 128], bbc[:, h, :], ALU.add),
                         reads=bk.res + [misc_res], writes=[gvt_res])
                S.op("dve", f_tt(mixedT[:, h, :], gvt[:, :], guT[:, h, :], ALU.mult), reads=[gvt_res, guT_res[h]], writes=[mixedT_res[h]])

        def attention_gen(ti):
            items = []
            for h in range(8):
                for kb in range(4 * ti + 3, -1, -1):
                    items.append((h, kb))
            N = len(items)
            st = {}
            top_kb = 4 * ti + 3

            def s1(n):
                h, kb = items[n]
                a = kb - 4 * ti
                c0 = max(a, 0) * 128
                zb = zbanks.next()
                kc = h // 2
                S.op("pe", f_mm(zb.ap[:, c0:T], kT[:, kc, kb * 128:(kb + 1) * 128], qpad[:, h, c0:T], True, False, skip=True),
                     reads=[kT_res[kc][kb // 4], qpad_res[h]], writes=zb.res)
                if a >= 0:
                    S.op("pe", f_mm(zb.ap[:, c0:c0 + 128], identb, negmask, False, False, skip=True), reads=[R_const], writes=zb.res)
                E = Eb[n % 2]
                SP = SPb[n % 4]
                S.op("act", f_act(E.ap[:, c0:T], zb.ap[:, c0:T], AF.Exp), reads=zb.res, writes=E.res)
                st[n] = (zb, c0, SP, E)

            def s1b(n):
                zb, c0, SP, E = st[n]
                S.op("act", f_act(SP.ap[:, c0:T], E.ap[:, c0:T], AF.Ln, bias=biasc[:, 2:3]), reads=E.res + [R_const], writes=SP.res)
                st[n] = (zb, c0, SP)

            def s3(n):
                h, kb = items[n]
                zb, c0, SP = st[n]
                a = kb - 4 * ti
                first = kb == top_kb
                m = top_kb - kb
                RSi = RSb[m % 2]
                RSo = RSb[(m + 1) % 2]
                c1 = (a + 1) * 128 if a >= 0 else 0
                has_carry = (not first) and c1 < T
                S.op("pe", f_mm(zb.ap[:, c0:T], mneg, SP.ap[:, c0:T], False, not has_carry, skip=True), reads=SP.res + [R_const], writes=zb.res)
                if has_carry:
                    S.op("pe", f_mm(zb.ap[:, c1:T], negones[:, :], RSi.ap[:, c1:T], False, True, skip=True), reads=RSi.res + [R_const], writes=zb.res)
                A = Ab[n % 4]
                S.op("act", f_act(A.ap[:, c0:T], zb.ap[:, c0:T], AF.Exp), reads=zb.res, writes=A.res)
                if kb > 0:
                    if a >= 0:
                        S.op("pool", f_copy(RSo.ap[:, c0:c0 + 128], SP.ap[:, c0:c0 + 128]), reads=SP.res, writes=RSo.res)
                        if c1 < T:
                            S.op("pool", f_tt(RSo.ap[:, c1:T], RSi.ap[:, c1:T], SP.ap[:, c1:T], ALU.add), reads=SP.res + RSi.res, writes=RSo.res)
                    else:
                        S.op("pool", f_tt(RSo.ap, RSi.ap, SP.ap, ALU.add), reads=SP.res + RSi.res, writes=RSo.res)
                st[n] = (zb, c0, SP, A)

            def s5(n):
                h, kb = items[n]
                _, c0, _, A = st[n]
                avb = avbanks[h % 2]
                first = kb == top_kb
                hp = h // 2
                S.op("pe", f_mm(avb.ap[:, c0:T], vc[:, kb, hp * 128:(hp + 1) * 128], A.ap[:, c0:T], first, kb == 0, skip=True),
                     reads=[v_res[kb]] + A.res, writes=avb.res)
                if kb == 0:
                    r0 = (h % 2) * 64
                    S.op("dve", f_copy(mixedT[r0:r0 + 64, 4 + hp, :], avb.ap[r0:r0 + 64, :]), reads=avb.res, writes=[mixedT_res[4 + hp]])
                del st[n]

            for n in range(N + 4):
                if n < N:
                    s1(n)
                if 0 <= n - 2 < N:
                    s3(n - 2)
                if n < N:
                    s1b(n)
                if 0 <= n - 4 < N:
                    s5(n - 4)
                yield

        def mix_out(ti):
            hb = ti % 2
            hT = hTb[hb]
            hT_res = hT_resb[hb]
            for j in range(8):
                w = rg.get(("mo", j))
                bk = allbanks.next()
                proj8(w, mixedT, mixedT_res, bk)
                S.op("dve", f_tt(hT[:, j, :], bk.ap, hT[:, j, :], ALU.add), reads=bk.res + [hT_res[j]], writes=[hT_res[j]])

        def ple_gen(ti, bp, exp_sig):
            hb = ti % 2
            hT = hTb[hb]
            hT_res = hT_resb[hb]
            sts = []
            for tb in range(4):
                st = stage_next()
                r0 = ti * T + tb * 128
                S.op("sp", f_dma(st.ap[:, 0:256], p_d[r0:r0 + 128, :]), writes=st.res, dma=st.chan)
                sts.append(st)
            for kk in range(2):
                bk = bp.next()
                for tb in range(4):
                    S.op("pe", f_tr(bk.ap[:, tb * 128:(tb + 1) * 128], sts[tb].ap[:, kk * 128:(kk + 1) * 128], identf),
                         reads=sts[tb].res + [R_const], writes=bk.res)
                emit_copy(evac_eng(), pT[:, kk, :], bk.ap, bk.res, [pT_res[kk]])
                yield
            norm(3, hb, bp)
            yield
            for j in range(8):
                if j % 4 == 0:
                    w = rg.get(("pp", j // 4))
                wgt = rg.get(("pg", j))
                bg = bp.next()
                bpj = bp.next()
                proj8(wgt, xnT, xnT_res, bg)
                yield
                for kk in range(2):
                    c0 = kk * 512 + (j % 4) * 128
                    S.op("pe", f_mm(bpj.ap, w.ap[:, c0:c0 + 128], pT[:, kk, :], kk == 0, kk == 1), reads=w.res + [pT_res[kk]], writes=bpj.res)
                ef = efb[j % 2][:, :]
                er = [ef_res[j % 2]]
                if exp_sig:
                    S.op("act", f_act(ef, bg.ap, AF.Exp, scale=-1.0), reads=bg.res, writes=er)
                    S.op("act", f_act(ef, ef, AF.Ln, bias=biasc[:, 2:3]), reads=er + [R_const], writes=er)
                    S.op("act", f_act(ef, ef, AF.Exp, scale=-1.0), reads=er, writes=er)
                else:
                    S.op("act", f_act(ef, bg.ap, AF.Sigmoid), reads=bg.res, writes=er)
                S.op("dve", f_tt(ef, ef, bpj.ap, ALU.mult), reads=er + bpj.res, writes=er)
                S.op("pool", f_tt(hT[:, j, :], hT[:, j, :], ef, ALU.add), reads=er + [hT_res[j]], writes=[hT_res[j]])
                yield

        out_ops = []

        def store_out_gen(ti, bp):
            hb = ti % 2
            hT = hTb[hb]
            hT_res = hT_resb[hb]
            norm(4, hb, bp, final=True)
            yield
            for tb in range(4):
                r0 = ti * T + tb * 128
                for half in range(2):
                    bk = bp.next()
                    for kk in range(4):
                        k = half * 4 + kk
                        S.op("pe", f_tr(bk.ap[:, kk * 128:(kk + 1) * 128], hT[:, k, tb * 128:(tb + 1) * 128], identf),
                             reads=[hT_res[k], R_const], writes=bk.res)
                    st = stage_next()
                    emit_copy(evac_eng(), st.ap[:, :], bk.ap, bk.res, st.res)
                    o = S.op("sp", f_dma(out_d[r0:r0 + 128, half * 512:(half + 1) * 512], st.ap[:, :]), reads=st.res, dma=st.chan)
                    out_ops.append(o)
                    yield

        def back_rest_gen(ti, bp, inter):
            for _ in ffn_gen(1, 2, ti % 2, bp, inter):
                yield
            for _ in ple_gen(ti, bp, inter):
                yield
            for _ in store_out_gen(ti, bp):
                yield

        def interleave(key, ga, gb):
            if dry:
                na = run(ga)
                nb = run(gb)
                counts[key] = (na, nb)
                return
            na, nb = counts[key]
            mode["inter"] = True
            done_b = 0
            for i, _ in enumerate(ga):
                target = ((i + 1) * nb) // na
                while done_b < target:
                    next(gb)
                    done_b += 1
            for _ in gb:
                pass
            mode["inter"] = False

        run(front_gen(0, allbanks, False))
        for ti in range(NT):
            mix_front(ti)

            def bgen(ti=ti):
                if ti >= 1:
                    for _ in back_rest_gen(ti - 1, sidebanks, True):
                        yield
                if ti + 1 < NT:
                    for _ in front_gen(ti + 1, sidebanks, True):
                        yield

            if interleave_on:
                interleave(("a", ti), attention_gen(ti), bgen())
            else:
                run(attention_gen(ti))
                if ti >= 1:
                    run(back_rest_gen(ti - 1, allbanks, False))
                if ti + 1 < NT:
                    run(front_gen(ti + 1, allbanks, False))
            mix_out(ti)
        run(back_rest_gen(NT - 1, allbanks, False))
        if not dry:
            S.op("sp", lambda e: e.nop(), extra=out_ops[-NSTAGE:])

    dr = DryRing()
    program(DrySched(), dr, True)
    stream = dr.stream
    seen = set()
    first_occ = []
    for u in stream:
        first_occ.append(u not in seen)
        seen.add(u)
    conv_list = [L for L in range(len(stream)) if first_occ[L]]
    conv_pos = {L: i for i, L in enumerate(conv_list)}

    S = Sched()

    class Ring:
        def __init__(self):
            self.consumed = 0
            self.loaded = 0
            self.f32_loaded = 0

        def f32_load(self):
            i = self.f32_loaded
            if i >= len(conv_list):
                return
            uu = stream[conv_list[i]]
            src, a, bdim = unit_src(uu)
            ncols = a * bdim
            ft, fres = fst[i % NF]
            dst = ft[:, 0:ncols].rearrange("p (a b) -> p a b", a=a)
            S.op("sp", f_dma(dst, src), writes=fres, dma="f%d" % (i % NF))
            self.f32_loaded += 1

        def prime(self):
            for _ in range(NF):
                self.f32_load()

        def get(self, u):
            idx = self.consumed
            assert stream[idx] == u, (stream[idx], u)
            self.consumed += 1
            lim = min(idx + RING - 4, len(stream))
            while self.loaded < lim:
                L = self.loaded
                uu = stream[L]
                ui = uidx[uu]
                _, a, bdim = unit_src(uu)
                ncols = a * bdim
                sl = L % RING
                if first_occ[L]:
                    i = conv_pos[L]
                    while self.f32_loaded < i + 1:
                        self.f32_load()
                    ft, fres = fst[i % NF]
                    ce = cast_eng[i % 3]
                    if ce == "act":
                        S.op("act", f_acopy(ring[sl][:, 0:ncols], ft[:, 0:ncols]), reads=fres, writes=[ring_res[sl]])
                    else:
                        S.op(ce, f_copy(ring[sl][:, 0:ncols], ft[:, 0:ncols]), reads=fres, writes=[ring_res[sl]])
                    if NT > 1:
                        S.op("sp", f_dma(wscr[ui, :, 0:ncols], ring[sl][:, 0:ncols]), reads=[ring_res[sl]], writes=[wscr_res[ui]], dma="w%d" % sl)
                    while self.f32_loaded < min(i + 1 + NF, len(conv_list)):
                        self.f32_load()
                else:
                    S.op("sp", f_dma(ring[sl][:, 0:ncols], wscr[ui, :, 0:ncols]), reads=[wscr_res[ui]], writes=[ring_res[sl]], dma="w%d" % sl)
                self.loaded += 1
            sl = idx % RING
            return Buf(ring[sl], ring_res[sl])

    program(S, Ring(), False)
    with nc.Block() as block:
        S.emit(nc, block)
    return nc


_CACHE = {}


def _consts():
    c = np.zeros((128, 640), np.float32)
    i = np.arange(128)
    c[:, 0:128] = np.eye(128, dtype=np.float32)
    c[:, 128:256] = np.where(i[:, None] >= i[None, :], -1.0, 0.0)
    c[:, 256:384] = np.where(i[:, None] >= i[None, :], -30000.0, 0.0)
    c[:, 384:512] = np.where(i[None, :] <= i[:, None], 1.0, 0.0)
    c[:, 512:640] = 1.0
    return c


def make_in_maps(inputs, NT, cores):
    f = lambda a: np.ascontiguousarray(np.asarray(a, dtype=np.float32))
    S_TOK = NT * T
    norms = np.concatenate([
        f(inputs["ffn1_norm"])[0].reshape(8, 128),
        f(inputs["mix_norm"])[0].reshape(8, 128),
        f(inputs["ffn2_norm"])[0].reshape(8, 128),
        f(inputs["ple_norm"])[0].reshape(8, 128),
        f(inputs["final_norm"]).reshape(8, 128),
    ], axis=0)
    shared = {
        "norms": f(norms),
        "f1_w_in": f(inputs["ffn1_w_in"])[0],
        "f2_w_in": f(inputs["ffn2_w_in"])[0],
        "f1_w_out": f(inputs["ffn1_w_out"])[0],
        "f2_w_out": f(inputs["ffn2_w_out"])[0],
        "w_mix_in": f(inputs["w_mix_in"])[0],
        "w_mix_out": f(inputs["w_mix_out"])[0],
        "ple_w_gate": f(inputs["ple_w_gate"])[0],
        "ple_w_proj": f(inputs["ple_w_proj"])[0],
        "gmlp_v_norm": f(inputs["gmlp_v_norm"])[0],
        "gmlp_w_s": f(inputs["gmlp_w_s"])[0],
        "gmlp_b": f(inputs["gmlp_b"])[0].reshape(512),
        "consts": _consts(),
    }
    x = np.asarray(inputs["x"])
    p = np.asarray(inputs["p"])
    maps = []
    for b in cores:
        m = dict(shared)
        m["x"] = f(x[b, :S_TOK])
        m["p"] = f(p[0, b, :S_TOK])
        maps.append(m)
    return maps


def kernel(**inputs):
    NT = 8
    if NT not in _CACHE:
        _CACHE[NT] = build(NT)
    nc = _CACHE[NT]
    in_maps = make_in_maps(inputs, NT, list(range(8)))
    res = run_bass_kernel_spmd(nc, in_maps, core_ids=list(range(8)))
    out = np.stack([np.asarray(r["out"], dtype=np.float32) for r in res.results], axis=0)
    return out
```

```python
import numpy as np
import concourse.bass as bass
import concourse.mybir as mybir
from concourse.bass_utils import run_bass_kernel_spmd

F32 = mybir.dt.float32
BF16 = mybir.dt.bfloat16
AF = mybir.ActivationFunctionType
ALU = mybir.AluOpType

D = 1024
DFF = 2816
NFF = DFF // 128
T = 512
EPS = 1e-6
RING = 12
NSTAGE = 4
EARLY_TILES = 3
ENGS = ("pe", "act", "dve", "pool", "sp")


class Res:
    __slots__ = ("name", "w", "r")

    def __init__(self, name):
        self.name = name
        self.w = None
        self.r = {}


class Op:
    __slots__ = ("eng", "seq", "fn", "waits", "needed", "dma", "val")


class Sched:
    def __init__(self):
        self.q = {e: [] for e in ENGS}
        self.ncomp = {e: 0 for e in ENGS}
        self.obs = {e: {} for e in ENGS}
        self.dma_cnt = {}

    def op(self, eng, fn, reads=(), writes=(), dma=None, extra=()):
        o = Op()
        o.eng = eng
        o.fn = fn
        o.dma = dma
        o.needed = False
        o.val = 0
        waits = {}
        obs = self.obs[eng]

        def need(p):
            if p.dma is None:
                key = p.eng
                if p.eng == eng and eng == "pe" and dma is None:
                    return
            else:
                key = "dma:" + p.dma
            if obs.get(key, 0) >= p.seq:
                return
            cur = waits.get(key)
            if cur is None or cur.seq < p.seq:
                waits[key] = p

        for r in reads:
            if r.w is not None:
                need(r.w)
        for w in writes:
            if w.w is not None:
                need(w.w)
            for p in w.r.values():
                need(p)
        for p in extra:
            need(p)
        for key, p in waits.items():
            obs[key] = p.seq
            p.needed = True
        o.waits = list(waits.values())
        if dma is None:
            self.ncomp[eng] += 1
            o.seq = self.ncomp[eng]
            key = eng
        else:
            self.dma_cnt[dma] = self.dma_cnt.get(dma, 0) + 1
            o.seq = self.dma_cnt[dma]
            key = "dma:" + dma
        for r in reads:
            cur = r.r.get(key)
            if cur is None or cur.seq < o.seq:
                r.r[key] = o
        for w in writes:
            w.w = o
            w.r = {}
        self.q[eng].append(o)
        return o

    def emit(self, nc, block):
        sems = {e: nc.alloc_semaphore("s_" + e) for e in ENGS}
        dsems = {c: nc.alloc_semaphore("d_" + c) for c in self.dma_cnt}
        for e in ENGS:
            v = 0
            for o in self.q[e]:
                if o.dma is None and o.needed:
                    v += 1
                    o.val = v

        def runner(ename):
            def f(eng):
                for o in self.q[ename]:
                    for p in o.waits:
                        if p.dma is None:
                            eng.wait_ge(sems[p.eng], p.val)
                        else:
                            eng.wait_ge(dsems[p.dma], 16 * p.seq)
                    ins = o.fn(eng)
                    if o.dma is not None:
                        ins.then_inc(dsems[o.dma], 16)
                    elif o.needed:
                        ins.then_inc(sems[ename], 1)
            return f

        block.tensor(runner("pe"))
        block.scalar(runner("act"))
        block.vector(runner("dve"))
        block.gpsimd(runner("pool"))
        block.sync(runner("sp"))


class Buf:
    __slots__ = ("ap", "res", "chan")

    def __init__(self, ap, res, chan=None):
        self.ap = ap
        self.res = res if isinstance(res, list) else [res]
        self.chan = chan


def f_mm(out, lhsT, rhs, start, stop, skip=False):
    if skip:
        return lambda e: e.matmul(out, lhsT, rhs, start=start, stop=stop, skip_group_check=True)
    return lambda e: e.matmul(out, lhsT, rhs, start=start, stop=stop)


def f_tr(out, in_, ident):
    return lambda e: e.transpose(out, in_, ident)


def f_act(out, in_, func, bias=None, scale=None, accum_out=None):
    kw = {}
    if bias is not None:
        kw["bias"] = bias
    if scale is not None:
        kw["scale"] = scale
    if accum_out is not None:
        kw["accum_out"] = accum_out
    return lambda e: e.activation(out, in_, func, **kw)


def f_copy(out, in_):
    return lambda e: e.tensor_copy(out, in_)


def f_acopy(out, in_):
    return lambda e: e.activation(out, in_, AF.Copy)


def f_tt(out, in0, in1, op):
    return lambda e: e.tensor_tensor(out, in0, in1, op)


def f_ts(out, in0, s1, s2, op0, op1=None):
    if op1 is None:
        return lambda e: e.tensor_scalar(out, in0, s1, None, op0)
    return lambda e: e.tensor_scalar(out, in0, s1, s2, op0, op1)


def f_stt(out, in0, scalar, in1, op0, op1):
    return lambda e: e.scalar_tensor_tensor(out, in0, scalar, in1, op0, op1)


def f_memset(ap, v):
    return lambda e: e.memset(ap, v)


def f_dma(out, in_):
    return lambda e: e.dma_start(out=out, in_=in_)


def tile_unit_order():
    order = []
    for fi in (0, 1):
        pass
    def ffn_units(fi):
        u = []
        for c in range(NFF):
            u.append(("in", fi, c, 0))
            u.append(("in", fi, c, 1))
        for j in range(8):
            for piece in range(3):
                u.append(("out", fi, j, piece))
        return u
    order += ffn_units(0)
    for oc in range(4):
        order.append(("mi_u", oc))
    for kp in range(4):
        order.append(("mi_gv", kp))
    for qc in range(4):
        order.append(("mi_q", qc))
    for kc in range(4):
        order.append(("mi_k", kc))
    for kp in range(4):
        order.append(("mi_v", kp))
    for j in range(8):
        order.append(("mo", j))
    order += ffn_units(1)
    for j in range(8):
        if j % 4 == 0:
            order.append(("pp", j // 4))
        order.append(("pg", j))
    return order


def build(NT, interleave_on=True):
    nc = bass.Bass("TRN2", target_bir_lowering=False)
    S_TOK = NT * T

    def din(name, shape):
        return nc.dram_tensor(name, shape, F32, kind="ExternalInput").ap()

    x_d = din("x", [S_TOK, D])
    p_d = din("p", [S_TOK, 256])
    norms_d = din("norms", [40, 128])
    w_in_d = [din("f1_w_in", [D, 2 * DFF]), din("f2_w_in", [D, 2 * DFF])]
    w_out_d = [din("f1_w_out", [DFF, D]), din("f2_w_out", [DFF, D])]
    w_mi_d = din("w_mix_in", [D, 2560])
    w_mo_d = din("w_mix_out", [D, D])
    w_pg_d = din("ple_w_gate", [D, D])
    w_pp_d = din("ple_w_proj", [256, D])
    gvn_d = din("gmlp_v_norm", [512])
    gws_d = din("gmlp_w_s", [4, 128, 128])
    gb_d = din("gmlp_b", [512])
    cst_d = din("consts", [128, 640])
    out_d = nc.dram_tensor("out", [S_TOK, D], F32, kind="ExternalOutput").ap()

    order = tile_unit_order()
    NU = len(order)
    uidx = {u: i for i, u in enumerate(order)}
    wscr = nc.dram_tensor("wscr", [NU, 128, 1024], BF16).ap()

    def unit_src(u):
        kind = u[0]
        if kind == "in":
            _, fi, c, g = u
            v = w_in_d[fi].rearrange("(k p) n -> p k n", p=128)
            c0 = g * DFF + c * 128
            return v[:, :, c0:c0 + 128], 8, 128
        if kind == "out":
            _, fi, j, piece = u
            v = w_out_d[fi].rearrange("(c p) n -> p c n", p=128)
            c0 = piece * 8
            c1 = min(c0 + 8, NFF)
            return v[:, c0:c1, j * 128:(j + 1) * 128], c1 - c0, 128
        if kind in ("mi_u", "mi_q", "mi_k"):
            base = {"mi_u": 0, "mi_q": 1024, "mi_k": 1536}[kind]
            v = w_mi_d.rearrange("(k p) n -> p k n", p=128)
            c0 = base + u[1] * 128
            return v[:, :, c0:c0 + 128], 8, 128
        if kind in ("mi_gv", "mi_v"):
            base = {"mi_gv": 512, "mi_v": 2048}[kind]
            v = w_mi_d.rearrange("(k p) n -> p k n", p=128)
            kp = u[1]
            return v[:, 2 * kp:2 * kp + 2, base:base + 512], 2, 512
        if kind == "mo":
            v = w_mo_d.rearrange("(k p) n -> p k n", p=128)
            return v[:, :, u[1] * 128:(u[1] + 1) * 128], 8, 128
        if kind == "pg":
            v = w_pg_d.rearrange("(k p) n -> p k n", p=128)
            return v[:, :, u[1] * 128:(u[1] + 1) * 128], 8, 128
        if kind == "pp":
            v = w_pp_d.rearrange("(k p) n -> p k n", p=128)
            return v[:, :, u[1] * 512:(u[1] + 1) * 512], 2, 512
        raise ValueError(u)

    off = [(nc.sbuf_base + 63) // 64 * 64]
    top = nc.sbuf_top

    def salloc(name, shape, dtype, at=None):
        esz = 4 if dtype == F32 else 2
        n = 1
        for s in shape[1:]:
            n *= s
        nbytes = n * esz
        if at is None:
            o = off[0]
            off[0] += (nbytes + 63) // 64 * 64
            assert off[0] <= top, "SBUF overflow %d > %d" % (off[0], top)
        else:
            o = at
        return nc.alloc_sbuf_tensor_at(name, shape, dtype, offset=o), o

    cst, _ = salloc("cst", [128, 640], F32)
    cb, _ = salloc("cb", [128, 640], BF16)
    negones, _ = salloc("negones", [128, 128], BF16)
    graw, _ = salloc("graw", [40, 128], F32)
    g32, _ = salloc("g32", [128, 40], F32)
    gvg, _ = salloc("gvg", [128, 512], F32)
    bbc, _ = salloc("bbc", [128, 4, 128], F32)
    wsT, _ = salloc("wsT", [128, 4, 128], BF16)
    ssq, _ = salloc("ssq", [128, 4], F32)
    rs1, _ = salloc("rs1", [128, 4], F32)
    biasc, _ = salloc("biasc", [128, 4], F32)
    kT, kT_off = salloc("kT", [128, 4, 8 * T], BF16)
    vc, vc_off = salloc("vc", [128, 32, 512], BF16)
    qpad, _ = salloc("qpad", [128, 8, T], BF16)
    hTb = [salloc("hT%d" % i, [128, 8, T], F32)[0] for i in range(2)]
    xnT, _ = salloc("xnT", [128, 8, T], BF16)
    mixedT, _ = salloc("mixedT", [128, 8, T], BF16)
    pT, _ = salloc("pT", [128, 2, T], BF16)
    hid, scr_off = salloc("hid", [128, NFF, T], BF16)
    efb = [salloc("ef%d" % i, [128, T], F32)[0] for i in range(2)]
    sqb = [salloc("sq%d" % i, [128, T], BF16)[0] for i in range(2)]
    rr, _ = salloc("rr", [128, T], F32)
    stg0_off = off[0]
    stg = [salloc("stg%d" % i, [128, T], F32)[0] for i in range(NSTAGE)]
    wsm = salloc("wsm", [128, 4, 128], F32, at=stg0_off)[0]
    ring = [salloc("ring%d" % i, [128, 1024], BF16)[0] for i in range(RING)]
    Eb_t = [salloc("E%d" % i, [128, T], BF16)[0] for i in range(2)]
    SPb_t = [salloc("SP%d" % i, [128, T], BF16)[0] for i in range(4)]
    RSb_t = [salloc("RS%d" % i, [128, T], BF16)[0] for i in range(2)]
    Ab_t = [salloc("A%d" % i, [128, T], BF16)[0] for i in range(4)]
    Eb = [Buf(t[:, :], Res("E%d" % i)) for i, t in enumerate(Eb_t)]
    SPb = [Buf(t[:, :], Res("SP%d" % i)) for i, t in enumerate(SPb_t)]
    RSb = [Buf(t[:, :], Res("RS%d" % i)) for i, t in enumerate(RSb_t)]
    Ab = [Buf(t[:, :], Res("A%d" % i)) for i, t in enumerate(Ab_t)]
    scr_res = [Res("scr%d" % i) for i in range(NFF)]
    guT = salloc("guT", [128, 4, T], BF16, at=scr_off)[0]
    guT_res = [scr_res[i] for i in range(4)]
    gvn = salloc("gvn", [128, 4, T], BF16, at=scr_off + 4096)[0]
    gvn_res = [scr_res[4 + i] for i in range(4)]
    print("SBUF used %d of %d" % (off[0], top))

    banks = [nc.alloc_psum_tensor("bank%d" % i, [128, 512], F32) for i in range(8)]
    bank_res = [Res("bank%d" % i) for i in range(8)]

    R_const = Res("const")
    R_g = Res("g32")
    hT_resb = [[Res("hT%d_%d" % (i, k)) for k in range(8)] for i in range(2)]
    xnT_res = [Res("xnT%d" % k) for k in range(8)]
    mixedT_res = [Res("mixedT%d" % k) for k in range(8)]
    qpad_res = [Res("qpad%d" % k) for k in range(8)]
    kT_res = [[Res("kT%d_%d" % (k, t)) for t in range(8)] for k in range(4)]
    v_res = [Res("v%d" % b) for b in range(32)]
    pT_res = [Res("pT%d" % k) for k in range(2)]
    ef_res = [Res("ef%d" % i) for i in range(2)]
    gvt = efb[0]
    gvt_res = ef_res[0]
    sq_res = [Res("sq%d" % i) for i in range(2)]
    rr_res = Res("rr")
    stg_res = [Res("stg%d" % i) for i in range(NSTAGE)]
    ring_res = [Res("ring%d" % i) for i in range(RING)]
    ssq_res = Res("ssq")
    rs1_res = Res("rs1")
    misc_res = Res("misc")
    wscr_res = [Res("wscr%d" % i) for i in range(NU)]

    identf = cst[:, 0:128]
    wsmask = cst[:, 384:512]
    identb = cb[:, 0:128]
    mneg = cb[:, 128:256]
    negmask = cb[:, 256:384]
    ones_bf = cb[:, 512:640]

    NF = 8
    fst = []
    for i in range(4):
        fst.append((salloc("fstk%d" % i, [128, 1024], F32, at=kT_off + i * 8192 + 4096)[0], [kT_res[i][t] for t in range(4, 8)]))
    for i in range(4):
        fst.append((salloc("fstv%d" % i, [128, 1024], F32, at=vc_off + (16 + 4 * i) * 1024)[0], [v_res[16 + 4 * i + t] for t in range(4)]))
    cast_eng = ("dve", "pool", "act")

    class DrySched:
        def op(self, *a, **k):
            return None

    class DryRing:
        def __init__(self):
            self.stream = []

        def get(self, u):
            self.stream.append(u)
            return Buf(ring[0], ring_res[0])

        def prime(self):
            pass

    counts = {}

    def program(S, rg, dry):
        class BankPool:
            def __init__(self, ids):
                self.ids = ids
                self.i = 0

            def next(self):
                b = self.ids[self.i % len(self.ids)]
                self.i += 1
                return Buf(banks[b][:, :], bank_res[b])

        allbanks = BankPool(list(range(8)))
        pools = {}

        def set_pools(ti):
            if ti < EARLY_TILES:
                pools["z"] = BankPool([0, 1, 2])
                pools["av"] = Buf(banks[3][:, :], bank_res[3])
                pools["side"] = BankPool([4, 5, 6, 7])
            else:
                pools["z"] = BankPool([0, 1, 2, 3])
                pools["av"] = Buf(banks[4][:, :], bank_res[4])
                pools["side"] = BankPool([5, 6, 7])

        set_pools(99)
        stage_i = [0]
        evac_i = [0]

        def stage_next():
            i = stage_i[0] % NSTAGE
            stage_i[0] += 1
            return Buf(stg[i], stg_res[i], "s%d" % i)

        mode = {"inter": False}

        def evac_eng():
            evac_i[0] += 1
            if mode["inter"]:
                return "dve"
            return "act" if evac_i[0] % 2 else "dve"

        def emit_copy(eng, out, in_, reads, writes):
            if eng == "act":
                S.op("act", f_acopy(out, in_), reads=reads, writes=writes)
            else:
                S.op(eng, f_copy(out, in_), reads=reads, writes=writes)

        S.op("sp", f_dma(cst[:, :], cst_d[:, :]), writes=[R_const], dma="c0")
        S.op("sp", f_dma(graw[:, :], norms_d[:, :]), writes=[R_g], dma="c1")
        S.op("sp", f_dma(gvg[:, :], gvn_d.partition_broadcast(128)), writes=[misc_res], dma="c2")
        S.op("sp", f_dma(bbc[:, :, :].rearrange("p h t -> p (h t)"), gb_d.partition_broadcast(128)), writes=[misc_res], dma="c3")
        S.op("sp", f_dma(wsm[:, :, :], gws_d.rearrange("h t s -> t h s")), writes=[stg_res[0]], dma="s0")
        rg.prime()
        S.op("dve", f_copy(cb[:, :], cst[:, :]), reads=[R_const], writes=[R_const])
        S.op("dve", f_ts(negones[:, :], cst[:, 512:640], -1.0, None, ALU.mult), reads=[R_const], writes=[R_const])
        S.op("dve", f_memset(biasc[:, 0:1], 1024.0 * EPS), writes=[R_const])
        S.op("dve", f_memset(biasc[:, 1:2], 512.0 * EPS), writes=[R_const])
        S.op("dve", f_memset(biasc[:, 2:3], 1.0), writes=[R_const])
        S.op("pool", f_memset(qpad[:, :, :].rearrange("p h t -> p (h t)"), 0.0), writes=qpad_res)
        b = allbanks.next()
        S.op("pe", f_tr(b.ap[:, 0:40], graw[:, :], cst[0:40, 0:40]), reads=[R_g, R_const], writes=b.res)
        S.op("dve", f_ts(g32[:, :], b.ap[:, 0:40], 32.0, None, ALU.mult), reads=b.res, writes=[R_g])
        S.op("dve", f_ts(gvg[:, :], gvg[:, :], float(np.sqrt(512.0)), None, ALU.mult), reads=[misc_res], writes=[misc_res])
        for h in range(4):
            S.op("dve", f_tt(wsm[:, h, :], wsm[:, h, :], wsmask, ALU.mult), reads=[stg_res[0], R_const], writes=[stg_res[0]])
        b = allbanks.next()
        for h in range(4):
            S.op("pe", f_tr(b.ap[:, h * 128:(h + 1) * 128], wsm[:, h, :], identf), reads=[stg_res[0], R_const], writes=b.res)
        S.op("dve", f_copy(wsT[:, :, :].rearrange("p h t -> p (h t)"), b.ap), reads=b.res, writes=[misc_res])
        stage_i[0] = 1

        def norm(ni, hb, bp, final=False):
            hT = hTb[hb]
            hT_res = hT_resb[hb]
            ssb = bp.next()
            for k in range(8):
                S.op("act", f_act(sqb[k % 2][:, :], hT[:, k, :], AF.Square), reads=[hT_res[k]], writes=[sq_res[k % 2]])
                S.op("pe", f_mm(ssb.ap, ones_bf, sqb[k % 2][:, :], k == 0, k == 7), reads=[sq_res[k % 2], R_const], writes=ssb.res)
            S.op("act", f_act(rr[:, :], ssb.ap, AF.Ln, bias=biasc[:, 0:1]), reads=ssb.res + [R_const], writes=[rr_res])
            S.op("act", f_act(rr[:, :], rr[:, :], AF.Exp, scale=-0.5), reads=[rr_res], writes=[rr_res])
            for k in range(8):
                gcol = g32[:, ni * 8 + k:ni * 8 + k + 1]
                if final:
                    S.op("dve", f_stt(hT[:, k, :], hT[:, k, :], gcol, rr[:, :], ALU.mult, ALU.mult),
                         reads=[hT_res[k], rr_res, R_g], writes=[hT_res[k]])
                else:
                    S.op("dve", f_stt(xnT[:, k, :], hT[:, k, :], gcol, rr[:, :], ALU.mult, ALU.mult),
                         reads=[hT_res[k], rr_res, R_g], writes=[xnT_res[k]])

        def proj8(w, src, src_res, bank):
            for k in range(8):
                S.op("pe", f_mm(bank.ap, w.ap[:, k * 128:(k + 1) * 128], src[:, k, :], k == 0, k == 7),
                     reads=w.res + [src_res[k]], writes=bank.res)

        def ffn_gen(fi, ni, hb, bp, exp_silu):
            hT = hTb[hb]
            hT_res = hT_resb[hb]
            norm(ni, hb, bp)
            yield
            for c in range(NFF):
                wg = rg.get(("in", fi, c, 0))
                wu = rg.get(("in", fi, c, 1))
                bg = bp.next()
                bu = bp.next()
                for wq, bq in ((wg, bg), (wu, bu)):
                    for k in range(8):
                        S.op("pe", f_mm(bq.ap, wq.ap[:, k * 128:(k + 1) * 128], xnT[:, k, :], k == 0, k == 7),
                             reads=wq.res + [xnT_res[k]], writes=bq.res)
                        if k % 4 == 3 and not (wq is wu and k == 7):
                            yield
                i = c % 2
                ef = efb[i][:, :]
                er = [ef_res[i]]
                if exp_silu:
                    S.op("act", f_act(ef, bg.ap, AF.Exp, scale=-1.0), reads=bg.res, writes=er)
                    S.op("act", f_act(ef, ef, AF.Ln, bias=biasc[:, 2:3]), reads=er + [R_const], writes=er)
                    S.op("act", f_act(ef, ef, AF.Exp, scale=-1.0), reads=er, writes=er)
                    S.op("dve", f_tt(ef, ef, bg.ap, ALU.mult), reads=er + bg.res, writes=er)
                else:
                    S.op("act", f_act(ef, bg.ap, AF.Silu), reads=bg.res, writes=er)
                S.op("dve", f_tt(hid[:, c, :], ef, bu.ap, ALU.mult), reads=er + bu.res, writes=[scr_res[c]])
                yield
            for j in range(8):
                by = bp.next()
                for piece in range(3):
                    w = rg.get(("out", fi, j, piece))
                    c0 = piece * 8
                    c1 = min(c0 + 8, NFF)
                    for c in range(c0, c1):
                        ci = c - c0
                        S.op("pe", f_mm(by.ap, w.ap[:, ci * 128:(ci + 1) * 128], hid[:, c, :], c == 0, c == NFF - 1),
                             reads=w.res + [scr_res[c]], writes=by.res)
                        if c % 4 == 3:
                            yield
                S.op("dve", f_stt(hT[:, j, :], by.ap, 0.5, hT[:, j, :], ALU.mult, ALU.add), reads=by.res + [hT_res[j]], writes=[hT_res[j]])
                yield

        def load_x_gen(ti, bp):
            hb = ti % 2
            hT = hTb[hb]
            hT_res = hT_resb[hb]
            for tb in range(4):
                r0 = ti * T + tb * 128
                for half in range(2):
                    st = stage_next()
                    S.op("sp", f_dma(st.ap[:, :], x_d[r0:r0 + 128, half * 512:(half + 1) * 512]), writes=st.res, dma=st.chan)
                    bk = bp.next()
                    for kk in range(4):
                        S.op("pe", f_tr(bk.ap[:, kk * 128:(kk + 1) * 128], st.ap[:, kk * 128:(kk + 1) * 128], identf),
                             reads=st.res + [R_const], writes=bk.res)
                    emit_copy(evac_eng(), hT[:, half * 4:(half + 1) * 4, tb * 128:(tb + 1) * 128],
                              bk.ap.rearrange("p (a b) -> p a b", a=4), bk.res, [hT_res[half * 4 + kk] for kk in range(4)])
                    yield

        def front_gen(ti, bp, exp_silu):
            for _ in load_x_gen(ti, bp):
                yield
            for _ in ffn_gen(0, 0, ti % 2, bp, exp_silu):
                yield
            norm(1, ti % 2, bp)
            yield

        def run(gen):
            n = 0
            for _ in gen:
                n += 1
            return n

        def mix_front(ti):
            hb = ti % 2
            bp = allbanks
            for oc in range(4):
                w = rg.get(("mi_u", oc))
                bk = bp.next()
                proj8(w, xnT, xnT_res, bk)
                S.op("act", f_act(guT[:, oc, :], bk.ap, AF.Gelu), reads=bk.res, writes=[guT_res[oc]])
            wv = [rg.get(("mi_gv", kp)) for kp in range(4)]
            for tb in range(4):
                bk = bp.next()
                for k in range(8):
                    w = wv[k // 2]
                    S.op("pe", f_mm(bk.ap, xnT[:, k, tb * 128:(tb + 1) * 128], w.ap[:, (k % 2) * 512:(k % 2 + 1) * 512], k == 0, k == 7),
                         reads=w.res + [xnT_res[k]], writes=bk.res)
                S.op("act", f_act(gvn[:, tb, :], bk.ap, AF.Gelu), reads=bk.res, writes=[gvn_res[tb]])
                S.op("act", f_act(sqb[tb % 2][:, :], gvn[:, tb, :], AF.Square, accum_out=ssq[:, tb:tb + 1]),
                     reads=[gvn_res[tb]], writes=[sq_res[tb % 2], ssq_res])
            S.op("act", f_act(rs1[:, 0:4], ssq[:, 0:4], AF.Ln, bias=biasc[:, 1:2]), reads=[ssq_res, R_const], writes=[rs1_res])
            S.op("act", f_act(rs1[:, 0:4], rs1[:, 0:4], AF.Exp, scale=-0.5), reads=[rs1_res], writes=[rs1_res])
            for tb in range(4):
                S.op("dve", f_stt(gvn[:, tb, :], gvn[:, tb, :], rs1[:, tb:tb + 1], gvg[:, :], ALU.mult, ALU.mult),
                     reads=[gvn_res[tb], rs1_res, misc_res], writes=[gvn_res[tb]])
            for qc in range(4):
                w = rg.get(("mi_q", qc))
                bk = bp.next()
                proj8(w, xnT, xnT_res, bk)
                S.op("act", f_act(qpad[0:64, 2 * qc, :], bk.ap[0:64, :], AF.Copy, scale=0.125), reads=bk.res, writes=[qpad_res[2 * qc]])
                S.op("dve", f_ts(qpad[64:128, 2 * qc + 1, :], bk.ap[64:128, :], 0.125, None, ALU.mult), reads=bk.res, writes=[qpad_res[2 * qc + 1]])
            for kc in range(4):
                w = rg.get(("mi_k", kc))
                bk = bp.next()
                proj8(w, xnT, xnT_res, bk)
                emit_copy(evac_eng(), kT[:, kc, ti * T:(ti + 1) * T], bk.ap, bk.res, [kT_res[kc][ti]])
            wv = [rg.get(("mi_v", kp)) for kp in range(4)]
            for tb in range(4):
                bk = bp.next()
                for k in range(8):
                    w = wv[k // 2]
                    S.op("pe", f_mm(bk.ap, xnT[:, k, tb * 128:(tb + 1) * 128], w.ap[:, (k % 2) * 512:(k % 2 + 1) * 512], k == 0, k == 7),
                         reads=w.res + [xnT_res[k]], writes=bk.res)
                emit_copy(evac_eng(), vc[:, ti * 4 + tb, :], bk.ap, bk.res, [v_res[ti * 4 + tb]])
            for h in range(4):
                bk = bp.next()
                for tb in range(4):
                    S.op("pe", f_mm(bk.ap[:, tb * 128:(tb + 1) * 128], gvn[:, tb, h * 128:(h + 1) * 128], wsT[:, h, :], tb == 0, tb == 3),
                         reads=[gvn_res[tb], misc_res], writes=bk.res)
                for tb in range(4):
                    S.op("dve", f_tt(gvt[:, tb * 128:(tb + 1) * 128], bk.ap[:, tb * 128:(tb + 1) * 128], bbc[:, h, :], ALU.add),
                         reads=bk.res + [misc_res], writes=[gvt_res])
                S.op("dve", f_tt(mixedT[:, h, :], gvt[:, :], guT[:, h, :], ALU.mult), reads=[gvt_res, guT_res[h]], writes=[mixedT_res[h]])

        def attention_gen(ti):
            items = []
            for h in range(8):
                for kb in range(4 * ti + 3, -1, -1):
                    items.append((h, kb))
            N = len(items)
            st = {}
            top_kb = 4 * ti + 3

            def s1(n):
                h, kb = items[n]
                a = kb - 4 * ti
                c0 = max(a, 0) * 128
                zb = pools["z"].next()
                kc = h // 2
                S.op("pe", f_mm(zb.ap[:, c0:T], kT[:, kc, kb * 128:(kb + 1) * 128], qpad[:, h, c0:T], True, False, skip=True),
                     reads=[kT_res[kc][kb // 4], qpad_res[h]], writes=zb.res)
                if a >= 0:
                    S.op("pe", f_mm(zb.ap[:, c0:c0 + 128], identb, negmask, False, False, skip=True), reads=[R_const], writes=zb.res)
                E = Eb[n % 2]
                SP = SPb[n % 4]
                S.op("act", f_act(E.ap[:, c0:T], zb.ap[:, c0:T], AF.Exp), reads=zb.res, writes=E.res)
                st[n] = (zb, c0, SP, E)

            def s1b(n):
                zb, c0, SP, E = st[n]
                S.op("act", f_act(SP.ap[:, c0:T], E.ap[:, c0:T], AF.Ln, bias=biasc[:, 2:3]), reads=E.res + [R_const], writes=SP.res)
                st[n] = (zb, c0, SP)

            def s3(n):
                h, kb = items[n]
                zb, c0, SP = st[n]
                a = kb - 4 * ti
                first = kb == top_kb
                m = top_kb - kb
                RSi = RSb[m % 2]
                RSo = RSb[(m + 1) % 2]
                c1 = (a + 1) * 128 if a >= 0 else 0
                has_carry = (not first) and c1 < T
                S.op("pe", f_mm(zb.ap[:, c0:T], mneg, SP.ap[:, c0:T], False, not has_carry, skip=True), reads=SP.res + [R_const], writes=zb.res)
                if has_carry:
                    S.op("pe", f_mm(zb.ap[:, c1:T], negones[:, :], RSi.ap[:, c1:T], False, True, skip=True), reads=RSi.res + [R_const], writes=zb.res)
                A = Ab[n % 4]
                S.op("act", f_act(A.ap[:, c0:T], zb.ap[:, c0:T], AF.Exp), reads=zb.res, writes=A.res)
                if kb > 0:
                    if a >= 0:
                        S.op("pool", f_copy(RSo.ap[:, c0:c0 + 128], SP.ap[:, c0:c0 + 128]), reads=SP.res, writes=RSo.res)
                        if c1 < T:
                            S.op("pool", f_tt(RSo.ap[:, c1:T], RSi.ap[:, c1:T], SP.ap[:, c1:T], ALU.add), reads=SP.res + RSi.res, writes=RSo.res)
                    else:
                        S.op("pool", f_tt(RSo.ap, RSi.ap, SP.ap, ALU.add), reads=SP.res + RSi.res, writes=RSo.res)
                st[n] = (zb, c0, SP, A)

            def s5(n):
                h, kb = items[n]
                _, c0, _, A = st[n]
                avb = pools["av"]
                first = kb == top_kb
                hp = h // 2
                S.op("pe", f_mm(avb.ap[:, c0:T], vc[:, kb, hp * 128:(hp + 1) * 128], A.ap[:, c0:T], first, kb == 0, skip=True),
                     reads=[v_res[kb]] + A.res, writes=avb.res)
                if kb == 0:
                    r0 = (h % 2) * 64
                    S.op("dve", f_copy(mixedT[r0:r0 + 64, 4 + hp, :], avb.ap[r0:r0 + 64, :]), reads=avb.res, writes=[mixedT_res[4 + hp]])
                del st[n]

            for n in range(N + 4):
                if n < N:
                    s1(n)
                if 0 <= n - 2 < N:
                    s3(n - 2)
                if n < N:
                    s1b(n)
                if 0 <= n - 4 < N:
                    s5(n - 4)
                yield

        def mix_out(ti):
            hb = ti % 2
            hT = hTb[hb]
            hT_res = hT_resb[hb]
            for j in range(8):
                w = rg.get(("mo", j))
                bk = allbanks.next()
                proj8(w, mixedT, mixedT_res, bk)
                S.op("dve", f_tt(hT[:, j, :], bk.ap, hT[:, j, :], ALU.add), reads=bk.res + [hT_res[j]], writes=[hT_res[j]])

        def ple_gen(ti, bp, exp_sig):
            hb = ti % 2
            hT = hTb[hb]
            hT_res = hT_resb[hb]
            sts = []
            for tb in range(4):
                st = stage_next()
                r0 = ti * T + tb * 128
                S.op("sp", f_dma(st.ap[:, 0:256], p_d[r0:r0 + 128, :]), writes=st.res, dma=st.chan)
                sts.append(st)
            for kk in range(2):
                bk = bp.next()
                for tb in range(4):
                    S.op("pe", f_tr(bk.ap[:, tb * 128:(tb + 1) * 128], sts[tb].ap[:, kk * 128:(kk + 1) * 128], identf),
                         reads=sts[tb].res + [R_const], writes=bk.res)
                emit_copy(evac_eng(), pT[:, kk, :], bk.ap, bk.res, [pT_res[kk]])
                yield
            norm(3, hb, bp)
            yield
            for j in range(8):
                if j % 4 == 0:
                    w = rg.get(("pp", j // 4))
                wgt = rg.get(("pg", j))
                bg = bp.next()
                bpj = bp.next()
                proj8(wgt, xnT, xnT_res, bg)
                yield
                for kk in range(2):
                    c0 = kk * 512 + (j % 4) * 128
                    S.op("pe", f_mm(bpj.ap, w.ap[:, c0:c0 + 128], pT[:, kk, :], kk == 0, kk == 1), reads=w.res + [pT_res[kk]], writes=bpj.res)
                ef = efb[j % 2][:, :]
                er = [ef_res[j % 2]]
                if exp_sig:
                    S.op("act", f_act(ef, bg.ap, AF.Exp, scale=-1.0), reads=bg.res, writes=er)
                    S.op("act", f_act(ef, ef, AF.Ln, bias=biasc[:, 2:3]), reads=er + [R_const], writes=er)
                    S.op("act", f_act(ef, ef, AF.Exp, scale=-1.0), reads=er, writes=er)
                else:
                    S.op("act", f_act(ef, bg.ap, AF.Sigmoid), reads=bg.res, writes=er)
                S.op("dve", f_tt(ef, ef, bpj.ap, ALU.mult), reads=er + bpj.res, writes=er)
                S.op("pool", f_tt(hT[:, j, :], hT[:, j, :], ef, ALU.add), reads=er + [hT_res[j]], writes=[hT_res[j]])
                yield

        out_ops = []

        def store_out_gen(ti, bp):
            hb = ti % 2
            hT = hTb[hb]
            hT_res = hT_resb[hb]
            norm(4, hb, bp, final=True)
            yield
            for tb in range(4):
                r0 = ti * T + tb * 128
                for half in range(2):
                    bk = bp.next()
                    for kk in range(4):
                        k = half * 4 + kk
                        S.op("pe", f_tr(bk.ap[:, kk * 128:(kk + 1) * 128], hT[:, k, tb * 128:(tb + 1) * 128], identf),
                             reads=[hT_res[k], R_const], writes=bk.res)
                    st = stage_next()
                    emit_copy(evac_eng(), st.ap[:, :], bk.ap, bk.res, st.res)
                    o = S.op("sp", f_dma(out_d[r0:r0 + 128, half * 512:(half + 1) * 512], st.ap[:, :]), reads=st.res, dma=st.chan)
                    out_ops.append(o)
                    yield

        def back_rest_gen(ti, bp, inter):
            for _ in ffn_gen(1, 2, ti % 2, bp, inter):
                yield
            for _ in ple_gen(ti, bp, inter):
                yield
            for _ in store_out_gen(ti, bp):
                yield

        def interleave(key, ga, gb):
            if dry:
                na = run(ga)
                nb = run(gb)
                counts[key] = (na, nb)
                return
            na, nb = counts[key]
            mode["inter"] = True
            done_b = 0
            for i, _ in enumerate(ga):
                target = ((i + 1) * nb) // na
                while done_b < target:
                    next(gb)
                    done_b += 1
            for _ in gb:
                pass
            mode["inter"] = False

        run(front_gen(0, allbanks, False))
        for ti in range(NT):
            mix_front(ti)

            def bgen(ti=ti):
                if ti >= 1:
                    for _ in back_rest_gen(ti - 1, pools["side"], True):
                        yield
                if ti + 1 < NT:
                    for _ in front_gen(ti + 1, pools["side"], True):
                        yield

            if interleave_on:
                set_pools(ti)
                interleave(("a", ti), attention_gen(ti), bgen())
            else:
                run(attention_gen(ti))
                if ti >= 1:
                    run(back_rest_gen(ti - 1, allbanks, False))
                if ti + 1 < NT:
                    run(front_gen(ti + 1, allbanks, False))
            mix_out(ti)
        run(back_rest_gen(NT - 1, allbanks, False))
        if not dry:
            S.op("sp", lambda e: e.nop(), extra=out_ops[-NSTAGE:])

    dr = DryRing()
    program(DrySched(), dr, True)
    stream = dr.stream
    seen = set()
    first_occ = []
    for u in stream:
        first_occ.append(u not in seen)
        seen.add(u)
    conv_list = [L for L in range(len(stream)) if first_occ[L]]
    conv_pos = {L: i for i, L in enumerate(conv_list)}

    S = Sched()

    class Ring:
        def __init__(self):
            self.consumed = 0
            self.loaded = 0
            self.f32_loaded = 0

        def f32_load(self):
            i = self.f32_loaded
            if i >= len(conv_list):
                return
            uu = stream[conv_list[i]]
            src, a, bdim = unit_src(uu)
            ncols = a * bdim
            ft, fres = fst[i % NF]
            dst = ft[:, 0:ncols].rearrange("p (a b) -> p a b", a=a)
            S.op("sp", f_dma(dst, src), writes=fres, dma="f%d" % (i % NF))
            self.f32_loaded += 1

        def prime(self):
            for _ in range(NF):
                self.f32_load()

        def get(self, u):
            idx = self.consumed
            assert stream[idx] == u, (stream[idx], u)
            self.consumed += 1
            lim = min(idx + RING - 4, len(stream))
            while self.loaded < lim:
                L = self.loaded
                uu = stream[L]
                ui = uidx[uu]
                _, a, bdim = unit_src(uu)
                ncols = a * bdim
                sl = L % RING
                if first_occ[L]:
                    i = conv_pos[L]
                    while self.f32_loaded < i + 1:
                        self.f32_load()
                    ft, fres = fst[i % NF]
                    ce = cast_eng[i % 3]
                    if ce == "act":
                        S.op("act", f_acopy(ring[sl][:, 0:ncols], ft[:, 0:ncols]), reads=fres, writes=[ring_res[sl]])
                    else:
                        S.op(ce, f_copy(ring[sl][:, 0:ncols], ft[:, 0:ncols]), reads=fres, writes=[ring_res[sl]])
                    if NT > 1:
                        S.op("sp", f_dma(wscr[ui, :, 0:ncols], ring[sl][:, 0:ncols]), reads=[ring_res[sl]], writes=[wscr_res[ui]], dma="w%d" % sl)
                    while self.f32_loaded < min(i + 1 + NF, len(conv_list)):
                        self.f32_load()
                else:
                    S.op("sp", f_dma(ring[sl][:, 0:ncols], wscr[ui, :, 0:ncols]), reads=[wscr_res[ui]], writes=[ring_res[sl]], dma="w%d" % sl)
                self.loaded += 1
            sl = idx % RING
            return Buf(ring[sl], ring_res[sl])

    program(S, Ring(), False)
    with nc.Block() as block:
        S.emit(nc, block)
    return nc


_CACHE = {}


def _consts():
    c = np.zeros((128, 640), np.float32)
    i = np.arange(128)
    c[:, 0:128] = np.eye(128, dtype=np.float32)
    c[:, 128:256] = np.where(i[:, None] >= i[None, :], -1.0, 0.0)
    c[:, 256:384] = np.where(i[:, None] >= i[None, :], -30000.0, 0.0)
    c[:, 384:512] = np.where(i[None, :] <= i[:, None], 1.0, 0.0)
    c[:, 512:640] = 1.0
    return c


def make_in_maps(inputs, NT, cores):
    f = lambda a: np.ascontiguousarray(np.asarray(a, dtype=np.float32))
    S_TOK = NT * T
    norms = np.concatenate([
        f(inputs["ffn1_norm"])[0].reshape(8, 128),
        f(inputs["mix_norm"])[0].reshape(8, 128),
        f(inputs["ffn2_norm"])[0].reshape(8, 128),
        f(inputs["ple_norm"])[0].reshape(8, 128),
        f(inputs["final_norm"]).reshape(8, 128),
    ], axis=0)
    shared = {
        "norms": f(norms),
        "f1_w_in": f(inputs["ffn1_w_in"])[0],
        "f2_w_in": f(inputs["ffn2_w_in"])[0],
        "f1_w_out": f(inputs["ffn1_w_out"])[0],
        "f2_w_out": f(inputs["ffn2_w_out"])[0],
        "w_mix_in": f(inputs["w_mix_in"])[0],
        "w_mix_out": f(inputs["w_mix_out"])[0],
        "ple_w_gate": f(inputs["ple_w_gate"])[0],
        "ple_w_proj": f(inputs["ple_w_proj"])[0],
        "gmlp_v_norm": f(inputs["gmlp_v_norm"])[0],
        "gmlp_w_s": f(inputs["gmlp_w_s"])[0],
        "gmlp_b": f(inputs["gmlp_b"])[0].reshape(512),
        "consts": _consts(),
    }
    x = np.asarray(inputs["x"])
    p = np.asarray(inputs["p"])
    maps = []
    for b in cores:
        m = dict(shared)
        m["x"] = f(x[b, :S_TOK])
        m["p"] = f(p[0, b, :S_TOK])
        maps.append(m)
    return maps


def kernel(**inputs):
    NT = 8
    if NT not in _CACHE:
        _CACHE[NT] = build(NT)
    nc = _CACHE[NT]
    in_maps = make_in_maps(inputs, NT, list(range(8)))
    res = run_bass_kernel_spmd(nc, in_maps, core_ids=list(range(8)))
    out = np.stack([np.asarray(r["out"], dtype=np.float32) for r in res.results], axis=0)
    return out
```
